# Optimizing a Trainium2 kernel written in Bass

```python
import math
import jax, jax.numpy as jnp
from jax import lax
import numpy as np

D_MODEL = 1024
BATCH = 16
SEQ = 2048
DEPTH = 2

HEAD_DIM = 64
N_HEADS = D_MODEL // HEAD_DIM
N_SB_HEADS = N_HEADS // 2
N_SWA_HEADS = N_HEADS - N_SB_HEADS
N_SWA_KV = 2
N_FOX_HEADS = N_HEADS
WINDOW = 128
BLOCK = 128
N_BUCKETS = 32
MAX_DISTANCE = 128
D_FF = 2816
ALPHA = (2 * DEPTH) ** 0.25
INIT_BETA = (8 * DEPTH) ** -0.25
LN_EPS = 1e-5
N_EVEN = (DEPTH + 1) // 2
N_ODD = DEPTH // 2

SB_W = N_SB_HEADS * HEAD_DIM
SWA_QW = N_SWA_HEADS * HEAD_DIM
SWA_KVW = N_SWA_KV * HEAD_DIM
AB_IN = 3 * SB_W + SWA_QW + 2 * SWA_KVW
AB_OUT = SB_W + SWA_QW
FOX_W = N_FOX_HEADS * HEAD_DIM
FOX_IN = 3 * FOX_W + N_FOX_HEADS

kernel_name = "hybrid_stickbreak_swa_fox_deepnorm"


def layer_norm(x, g, b):
    xf = x.astype(jnp.float32)
    mu = jnp.mean(xf, axis=-1, keepdims=True)
    var = jnp.mean(jnp.square(xf - mu), axis=-1, keepdims=True)
    return ((xf - mu) * lax.rsqrt(var + LN_EPS)).astype(x.dtype) * g + b


def swiglu_ffn(x, w_in, w_out):
    gate, up = jnp.split(x @ w_in, 2, axis=-1)
    return (jax.nn.silu(gate) * up) @ w_out


def t5_causal_bucket(rel):
    n = jnp.maximum(rel, 0)
    max_exact = N_BUCKETS // 2
    nf = jnp.maximum(n, 1).astype(jnp.float32)
    large = max_exact + (jnp.log(nf / max_exact) / math.log(MAX_DISTANCE / max_exact)
                         * (N_BUCKETS - max_exact)).astype(jnp.int32)
    large = jnp.minimum(large, N_BUCKETS - 1)
    return jnp.where(n < max_exact, n, large)


def stick_breaking_attention(q, k, v):
    S, Dh = q.shape[1], q.shape[3]
    scale = Dh ** -0.5
    outs = []
    for blk in range(S // BLOCK):
        q0, end = blk * BLOCK, (blk + 1) * BLOCK
        z = jnp.einsum('bqhd,bkhd->bhqk', q[:, q0:end], k[:, :end]).astype(jnp.float32) * scale
        t_pos = q0 + jnp.arange(BLOCK)[:, None]
        s_pos = jnp.arange(end)[None, :]
        strict = s_pos < t_pos
        log_keep = jnp.where(strict, jax.nn.log_sigmoid(-z), 0.0)
        after = lax.cumsum(log_keep, axis=3, reverse=True) - log_keep
        w = jnp.where(strict, jnp.exp(jax.nn.log_sigmoid(z) + after), 0.0)
        outs.append(jnp.einsum('bhqk,bkhd->bqhd', w.astype(v.dtype), v[:, :end]))
    return jnp.concatenate(outs, axis=1)


def sliding_window_sink_attention(q, k, v, sinks, rel_bias):
    B, S, Hq, Dh = q.shape
    Hkv = k.shape[2]
    G = Hq // Hkv
    nb = S // BLOCK
    qb = q.reshape(B, nb, BLOCK, Hkv, G, Dh)

    def band(t):
        tb = t.reshape(B, nb, BLOCK, Hkv, Dh)
        prev = jnp.pad(tb, ((0, 0), (1, 0), (0, 0), (0, 0), (0, 0)))[:, :-1]
        return jnp.concatenate([prev, tb], axis=2)

    kb, vb = band(k), band(v)
    logits = jnp.einsum('bnqhgd,bnkhd->bnhgqk', qb, kb).astype(jnp.float32) * Dh ** -0.5
    qi = jnp.arange(BLOCK)[:, None]
    kj = jnp.arange(2 * BLOCK)[None, :]
    rel = qi + BLOCK - kj
    bias = rel_bias[t5_causal_bucket(rel)].astype(jnp.float32)
    bias = bias.transpose(2, 0, 1).reshape(Hkv, G, BLOCK, 2 * BLOCK)
    in_window = (rel >= 0) & (rel < WINDOW)
    key_pos = jnp.arange(nb)[:, None, None] * BLOCK - BLOCK + kj[None]
    valid = in_window[None] & (key_pos >= 0)
    logits = jnp.where(valid[None, :, None, None], logits + bias, -jnp.inf)
    sink = sinks.astype(jnp.float32).reshape(Hkv, G, 1, 1)
    m = jnp.maximum(jnp.max(logits, axis=-1, keepdims=True), sink)
    p = jnp.exp(logits - m)
    w = p / (jnp.sum(p, axis=-1, keepdims=True) + jnp.exp(sink - m))
    out = jnp.einsum('bnhgqk,bnkhd->bnqhgd', w.astype(v.dtype), vb)
    return out.reshape(B, S, Hq, Dh)


def forgetting_attention(q, k, v, log_f):
    S, Dh = q.shape[1], q.shape[3]
    scale = Dh ** -0.5
    c = lax.cumsum(log_f, axis=1).transpose(0, 2, 1)
    outs = []
    for blk in range(S // BLOCK):
        q0, end = blk * BLOCK, (blk + 1) * BLOCK
        logits = jnp.einsum('bqhd,bkhd->bhqk', q[:, q0:end], k[:, :end]).astype(jnp.float32) * scale
        logits = logits + c[:, :, q0:end, None] - c[:, :, None, :end]
        causal = jnp.arange(end)[None, :] <= (q0 + jnp.arange(BLOCK)[:, None])
        w = jax.nn.softmax(jnp.where(causal, logits, -jnp.inf), axis=-1)
        outs.append(jnp.einsum('bhqk,bkhd->bqhd', w.astype(v.dtype), v[:, :end]))
    return jnp.concatenate(outs, axis=1)


def even_mixer(h, w_in, w_out, sinks, rel_bias):
    B, S, _ = h.shape
    proj = h @ w_in
    cuts = [SB_W, 2 * SB_W, 3 * SB_W, 3 * SB_W + SWA_QW, 3 * SB_W + SWA_QW + SWA_KVW]
    sb_q, sb_k, sb_v, sw_q, sw_k, sw_v = jnp.split(proj, cuts, axis=-1)
    heads = lambda t, n: t.reshape(B, S, n, HEAD_DIM)
    o_sb = stick_breaking_attention(heads(sb_q, N_SB_HEADS), heads(sb_k, N_SB_HEADS),
                                    heads(sb_v, N_SB_HEADS))
    o_sw = sliding_window_sink_attention(heads(sw_q, N_SWA_HEADS), heads(sw_k, N_SWA_KV),
                                         heads(sw_v, N_SWA_KV), sinks, rel_bias)
    o = jnp.concatenate([o_sb.reshape(B, S, SB_W), o_sw.reshape(B, S, SWA_QW)], axis=-1)
    return o @ w_out


def odd_mixer(h, w_in, b_f, w_out):
    B, S, _ = h.shape
    proj = h @ w_in
    q, k, v, f = jnp.split(proj, [FOX_W, 2 * FOX_W, 3 * FOX_W], axis=-1)
    heads = lambda t: t.reshape(B, S, N_FOX_HEADS, HEAD_DIM)
    log_f = jax.nn.log_sigmoid((f + b_f).astype(jnp.float32))
    o = forgetting_attention(heads(q), heads(k), heads(v), log_f)
    return o.reshape(B, S, FOX_W) @ w_out


def setup_inputs(seed: int = 0) -> dict:
    key = jax.random.key(seed)
    ks = jax.random.split(key, 16)
    nrm = lambda k, shape, s: jax.random.normal(k, shape, jnp.float32) * s
    return {
        "x": nrm(ks[0], (BATCH, SEQ, D_MODEL), 1.0),
        "ln_g": 1.0 + nrm(ks[1], (DEPTH, 3, D_MODEL), 0.05),
        "ln_b": nrm(ks[2], (DEPTH, 3, D_MODEL), 0.02),
        "ffn1_in": nrm(ks[3], (DEPTH, D_MODEL, 2 * D_FF), D_MODEL ** -0.5),
        "ffn1_out": nrm(ks[4], (DEPTH, D_FF, D_MODEL), D_FF ** -0.5 * INIT_BETA),
        "ffn2_in": nrm(ks[5], (DEPTH, D_MODEL, 2 * D_FF), D_MODEL ** -0.5),
        "ffn2_out": nrm(ks[6], (DEPTH, D_FF, D_MODEL), D_FF ** -0.5 * INIT_BETA),
        "ab_w_in": nrm(ks[7], (N_EVEN, D_MODEL, AB_IN), D_MODEL ** -0.5),
        "ab_w_out": nrm(ks[8], (N_EVEN, AB_OUT, D_MODEL), AB_OUT ** -0.5 * INIT_BETA),
        "ab_sinks": nrm(ks[9], (N_EVEN, N_SWA_HEADS), 0.1),
        "fox_w_in": nrm(ks[10], (N_ODD, D_MODEL, FOX_IN), D_MODEL ** -0.5),
        "fox_b_f": nrm(ks[11], (N_ODD, N_FOX_HEADS), 0.1),
        "fox_w_out": nrm(ks[12], (N_ODD, FOX_W, D_MODEL), FOX_W ** -0.5 * INIT_BETA),
        "rel_bias": nrm(ks[13], (N_BUCKETS, N_SWA_HEADS), 0.2),
    }


def reference(x, ln_g, ln_b, ffn1_in, ffn1_out, ffn2_in, ffn2_out, ab_w_in, ab_w_out,
              ab_sinks, fox_w_in, fox_b_f, fox_w_out, rel_bias):
    h = x
    for layer in range(DEPTH):
        h = layer_norm(ALPHA * h + 0.5 * swiglu_ffn(h, ffn1_in[layer], ffn1_out[layer]),
                       ln_g[layer, 0], ln_b[layer, 0])
        if layer % 2 == 0:
            i = layer // 2
            mix = even_mixer(h, ab_w_in[i], ab_w_out[i], ab_sinks[i], rel_bias)
        else:
            i = layer // 2
            mix = odd_mixer(h, fox_w_in[i], fox_b_f[i], fox_w_out[i])
        h = layer_norm(ALPHA * h + mix, ln_g[layer, 1], ln_b[layer, 1])
        h = layer_norm(ALPHA * h + 0.5 * swiglu_ffn(h, ffn2_in[layer], ffn2_out[layer]),
                       ln_g[layer, 2], ln_b[layer, 2])
    return h
```

```python
import os
import numpy as np
import ml_dtypes
from contextlib import ExitStack
import concourse.bass as bass
import concourse.mybir as mybir
from concourse.bass_utils import run_bass_kernel_spmd

F32 = mybir.dt.float32
BF16 = mybir.dt.bfloat16
AF = mybir.ActivationFunctionType
ALU = mybir.AluOpType

D = 1024
S = 2048
NCH = 8
DFF = 2816
NJ = 22
TT = 512
NT = S // TT
ALPHA = 4.0 ** 0.25
EPS = 1e-5
NEG = -30000.0
N_CORES = 8


class Buf:
    __slots__ = ("w", "r", "excl")

    def __init__(self, excl=False):
        self.w = None
        self.r = {}
        self.excl = excl


class Eng:
    def __init__(self, ctx, name, eng):
        self.ctx = ctx
        self.name = name
        self.eng = eng
        self.k = ctx.new_sem("s_" + name)
        self.cnt = 0
        self.seen = {}

    def wait(self, ev):
        if ev is None:
            return
        k, v = ev
        if self.seen.get(k, 0) >= v:
            return
        self.eng.wait_ge(self.ctx.sems[k], v)
        self.seen[k] = v


class Ctx:
    def __init__(self, nc, es):
        self.nc = nc
        self.es = es
        self.sems = []
        self.semval = []
        self.dry = False
        self.n_ins = 0

    def new_sem(self, name):
        h = self.es.enter_context(self.nc.semaphore(name))
        self.sems.append(h)
        self.semval.append(0)
        return len(self.sems) - 1

    def op(self, E, fns, R=(), W=()):
        if self.dry:
            return
        if not isinstance(fns, (list, tuple)):
            fns = [fns]
        W = list(W) + [b for b in R if b.excl]
        R = [b for b in R if not b.excl]
        for b in R:
            E.wait(b.w)
        for b in W:
            E.wait(b.w)
            for k, v in b.r.items():
                E.wait((k, v))
        ins = None
        for f in fns:
            ins = f()
            self.n_ins += 1
        E.cnt += 1
        ins.then_inc(self.sems[E.k], 1)
        ev = (E.k, E.cnt)
        for b in R:
            if b.r.get(E.k, 0) < E.cnt:
                b.r[E.k] = E.cnt
        for b in W:
            b.w = ev
            b.r = {}
        return ev

    def dma(self, Q, sk, out, in_, R=(), W=()):
        if self.dry:
            return
        for b in R:
            Q.wait(b.w)
        for b in W:
            Q.wait(b.w)
            for k, v in b.r.items():
                Q.wait((k, v))
        ins = Q.eng.dma_start(out=out, in_=in_)
        self.n_ins += 1
        self.semval[sk] += 16
        ins.then_inc(self.sems[sk], 16)
        ev = (sk, self.semval[sk])
        for b in R:
            if b.r.get(sk, 0) < ev[1]:
                b.r[sk] = ev[1]
        for b in W:
            b.w = ev
            b.r = {}
        return ev


class Rot:
    def __init__(self, n):
        self.n = n
        self.i = 0
        self.held = set()

    def get(self, hold=False):
        for _ in range(self.n + 1):
            s = self.i
            self.i = (self.i + 1) % self.n
            if s not in self.held:
                if hold:
                    self.held.add(s)
                return s
        raise RuntimeError("no free slot")

    def release(self, s):
        self.held.discard(s)


class Ring:
    def __init__(self, ctx, Q, nslots, bufs, name):
        self.ctx = ctx
        self.Q = Q
        self.n = nslots
        self.bufs = bufs
        self.sk = [ctx.new_sem("%s%d" % (name, i)) for i in range(nslots)]
        self.plan = []
        self.issued = 0
        self.pos = 0

    def reset(self):
        self.issued = 0
        self.pos = 0

    def get(self, loader, keep=0):
        if self.ctx.dry:
            self.plan.append(loader)
            return 0
        i = self.pos
        self.pos += 1
        while self.issued < len(self.plan) and self.issued < i + self.n - keep:
            k = self.issued
            sl = k % self.n
            for (o, a) in self.plan[k](sl):
                self.ctx.dma(self.Q, self.sk[sl], o, a, W=[self.bufs[sl]])
            self.issued += 1
        return i % self.n


def build(nseq=2, upto=99):
    nc = bass.Bass("TRN2", target_bir_lowering=False)
    dt = lambda n, s, d, k="ExternalInput": nc.dram_tensor(n, list(s), d, kind=k).ap()
    x_d = dt("x", [nseq, S, D], F32)
    out_d = dt("out", [nseq, S, D], F32, "ExternalOutput")
    fin_d = [dt("ffn1_in", [2, D, 2 * DFF], F32), dt("ffn2_in", [2, D, 2 * DFF], F32)]
    fout_d = [dt("ffn1_out", [2, DFF, D], F32), dt("ffn2_out", [2, DFF, D], F32)]
    abin_d = dt("ab_w_in", [1, D, 2304], F32)
    about_d = dt("ab_w_out", [1, D, D], F32)
    fxin_d = dt("fox_w_in", [1, D, 3088], F32)
    fxout_d = dt("fox_w_out", [1, D, D], F32)
    lnp_d = dt("lnp", [128, 96], F32)
    cf_d = dt("cf", [128, 8], F32)
    identf_d = dt("identf", [128, 128], F32)
    trile_d = dt("trile", [128, 128], F32)
    onesf_d = dt("onesf", [128, 128], F32)
    cbf_d = dt("cbf", [128, 4, 128], BF16)
    mask_d = dt("maskw", [128, 897], BF16)
    sel_d = dt("sel", [128, 16, 128], BF16)
    swab_d = dt("swab", [128, 2048], F32)
    sinkb_d = dt("sinkb", [128, 8], F32)
    bfb_d = dt("bfb", [128, 16], F32)

    swi_d = [[dt("swi%d%d" % (l, w), [NJ // 2, 128, 4096], BF16, "Internal") for w in range(2)] for l in range(2)]
    swo_d = [[dt("swo%d%d" % (l, w), [NCH, 128, NJ, 128], BF16, "Internal") for w in range(2)] for l in range(2)]
    NG_AB = 4 + 2
    NG_FX = 8 + 1
    swm_d = [dt("swm0", [NG_AB, 128, 3072], BF16, "Internal"),
             dt("swm1", [NG_FX, 128, 3072], BF16, "Internal")]
    swp_d = [dt("swp%d" % l, [NCH, 128, NCH, 128], BF16, "Internal") for l in range(2)]

    es = ExitStack()
    with es:
        ctx = Ctx(nc, es)
        sb = lambda n, s, d: es.enter_context(nc.sbuf_tensor(n, list(s), d))
        XRESf = sb("xres", [128, NCH * S], F32)
        XBf = sb("xb", [128, NCH * S], BF16)
        BIGAf = sb("biga", [128, 16384], BF16)
        QKBf = sb("qkb", [128, 4 * S], BF16)
        VAf = sb("va", [128, 16 * 256], BF16)
        TFf = sb("tf", [128, 6 * 512], F32)
        TBf = sb("tb", [128, 3 * 512], BF16)
        NWA = 2
        WAf = sb("wa", [128, NWA * 4096], BF16)
        NWB = 3
        WBf = sb("wb", [128, NWB * 11 * 128], BF16)
        LSf = sb("ls", [128, 2048], F32)
        lnp = sb("lnp_s", [128, 96], F32)
        cf = sb("cf_s", [128, 8], F32)
        identf = sb("identf_s", [128, 128], F32)
        cbf = sb("cbf_s", [128, 4, 128], BF16)
        maskw = sb("maskw_s", [128, 897], BF16)
        sel = sb("sel_s", [128, 16, 128], BF16)
        sinkb = sb("sinkb_s", [128, 8], F32)
        bfb = sb("bfb_s", [128, 16], F32)
        sinkexp = sb("sinkexp_s", [128, 8], F32)
        psum = [es.enter_context(nc.psum_tensor("ps%d" % i, [128, 512], F32)) for i in range(8)]

        XRES = XRESf[:].rearrange("p (c t) -> p c t", t=S)
        XB = XBf[:].rearrange("p (c t) -> p c t", t=S)
        GT = BIGAf[:, 0:NJ * 512].rearrange("p (j t) -> p j t", t=512)
        OT = BIGAf[:].rearrange("p (c t) -> p c t", t=S)
        QKB = QKBf[:].rearrange("p (c t) -> p c t", t=S)
        VA = VAf[:].rearrange("p (b h m) -> p b h m", h=2, m=128)
        TF = TFf[:].rearrange("p (n t) -> p n t", t=512)
        TB = TBf[:].rearrange("p (n t) -> p n t", t=512)
        WA = WAf[:].rearrange("p (n e) -> p n e", e=4096)
        WB = WBf[:].rearrange("p (n f m) -> p n f m", f=11, m=128)
        SWAB = LSf[:].rearrange("p (g k j t) -> p g k j t", g=2, k=2, j=4)
        NEGC = LSf[:, 0:256].rearrange("p (b h) -> p b h", h=16)
        LPRE = LSf[:, 256:512].rearrange("p (b h) -> p b h", h=16)
        LL = LSf[:, 512:768].rearrange("p (b h) -> p b h", h=16)
        onesf = LSf[:, 768:896]
        trile = LSf[:, 896:1024]
        CSf = LSf[:, 1024:2048].bitcast(BF16)
        ONESDIV, IDENTB, NEGTRI, ONESB = (cbf[:, i, :] for i in range(4))

        PE = Eng(ctx, "pe", nc.tensor)
        ACT = Eng(ctx, "act", nc.scalar)
        DVE = Eng(ctx, "dve", nc.vector)
        POOL = Eng(ctx, "pool", nc.gpsimd)
        SP = Eng(ctx, "sp", nc.sync)
        k_const = ctx.new_sem("k_const")
        k_xin = [ctx.new_sem("k_xin%d" % i) for i in range(3)]
        if os.environ.get('KDBG_XSEM'):
            k_xin = [k_const] * 3
        k_xout = [ctx.new_sem("k_xout%d" % i) for i in range(3)]
        k_cl = [ctx.new_sem("k_cl%d" % i) for i in range(2)]
        k_cs = [ctx.new_sem("k_cs%d" % i) for i in range(2)]
        k_ls = ctx.new_sem("k_ls")

        B_const = Buf()
        B_xres = [[Buf() for _ in range(NT)] for _ in range(NCH)]
        B_xb = [[Buf() for _ in range(NT)] for _ in range(NCH)]
        B_biga = [Buf() for _ in range(32)]
        B_qkb = [[Buf() for _ in range(NT)] for _ in range(4)]
        B_va = [Buf() for _ in range(16)]
        B_tf = [Buf() for _ in range(6)]
        B_tb = [Buf() for _ in range(3)]
        B_wa = [Buf() for _ in range(NWA)]
        B_wb = [Buf() for _ in range(NWB)]
        B_ps = [Buf(excl=True) for _ in range(8)]
        B_ls = Buf()
        B_cs = [Buf() for _ in range(NT)]
        B_misc = Buf()
        PS = Rot(8)
        TFR = Rot(6)
        TBR = Rot(3)
        ringA = Ring(ctx, SP, NWA, B_wa, "k_wa")
        ringB = Ring(ctx, SP, NWB, B_wb, "k_wb")

        op = ctx.op
        dma = ctx.dma
        mm = nc.tensor.matmul
        act = nc.scalar.activation

        def load_consts():
            for (o, a) in [(lnp, lnp_d), (cf, cf_d), (identf, identf_d), (cbf, cbf_d), (maskw, mask_d), (sel, sel_d),
                           (sinkb, sinkb_d), (bfb, bfb_d)]:
                dma(SP, k_const, o[:], a, W=[B_const])
            op(ACT, lambda: act(out=sinkexp[:], in_=sinkb[:], func=AF.Exp), R=[B_const], W=[B_misc])
            op(DVE, lambda: nc.vector.memset(VAf[:], 1.0), W=B_va)

        B_stf = [Buf(), Buf()]
        B_stb = [[Buf() for _ in range(8)] for _ in range(2)]
        conv_state = {"i": 0, "e": 0}
        cengs = None

        def conv(loads, copies, stores):
            i = conv_state["i"] % 2
            conv_state["i"] += 1
            Fs = XRESf[:, i * 8192:(i + 1) * 8192]
            Bs = XBf[:, i * 8192:(i + 1) * 8192]
            for (o, a) in loads(Fs):
                dma(SP, k_cl[i], o, a, W=[B_stf[i]])
            for pi, (o, a) in enumerate(copies(Fs, Bs)):
                e = conv_state["e"] % 3
                conv_state["e"] += 1
                wb_ = [B_stb[i][pi]]
                if e == 0:
                    op(POOL, lambda o=o, a=a: nc.gpsimd.tensor_copy(out=o, in_=a), R=[B_stf[i]], W=wb_)
                elif e == 1:
                    op(ACT, lambda o=o, a=a: act(out=o, in_=a, func=AF.Copy), R=[B_stf[i]], W=wb_)
                else:
                    op(DVE, lambda o=o, a=a: nc.vector.tensor_copy(out=o, in_=a), R=[B_stf[i]], W=wb_)
            for (o, a) in stores(Bs):
                dma(SP, k_cs[i], o, a, R=B_stb[i])

        def prologue():
            for l in range(2):
                for w in range(2):
                    src = fin_d[w][l].rearrange("(kc p) (g c) -> p kc g c", p=128, g=2)
                    for u in range(NJ // 2):
                        def loads(Fs, src=src, u=u):
                            o = Fs[:, 0:4096].rearrange("p (kc g c) -> p kc g c", kc=8, g=2)
                            return [(o[:, :, g_, :], src[:, :, g_, u * 256:(u + 1) * 256]) for g_ in range(2)]
                        def copies(Fs, Bs):
                            return [(Bs[:, h_ * 2048:(h_ + 1) * 2048], Fs[:, h_ * 2048:(h_ + 1) * 2048]) for h_ in range(2)]
                        def stores(Bs, l=l, w=w, u=u):
                            return [(swi_d[l][w][u], Bs[:, 0:4096])]
                        conv(loads, copies, stores)
            for l in range(2):
                for w in range(2):
                    src = fout_d[w][l].rearrange("(fc p) c -> p fc c", p=128)
                    for f0, nf in ((0, 8), (8, 8), (16, 6)):
                        def loads(Fs, src=src, f0=f0, nf=nf):
                            o = Fs[:, 0:nf * 1024].rearrange("p (f c) -> p f c", f=nf)
                            return [(o, src[:, f0:f0 + nf, :])]
                        def copies(Fs, Bs, nf=nf):
                            fi = Fs[:, 0:nf * 1024].rearrange("p (f d m) -> p f d m", f=nf, d=8)
                            bo = Bs[:, 0:nf * 1024].rearrange("p (d f m) -> p d f m", d=8, f=nf)
                            r = []
                            for d0 in range(0, 8, 2):
                                for dd in range(2):
                                    r.append((bo[:, d0 + dd, :, :], fi[:, :, d0 + dd, :]))
                            return r
                        def stores(Bs, l=l, w=w, f0=f0, nf=nf):
                            bo = Bs[:, 0:nf * 1024].rearrange("p (d f m) -> p d f m", d=8, f=nf)
                            return [(swo_d[l][w][:, :, f0:f0 + nf, :].rearrange("d p f m -> p d f m"), bo)]
                        conv(loads, copies, stores)
            abv = abin_d[0].rearrange("(kc p) c -> p kc c", p=128)
            fxv = fxin_d[0].rearrange("(kc p) c -> p kc c", p=128)
            swq = abin_d[0][:, 1536:2048].rearrange("(kc p) (g j e) -> p kc g j e", p=128, g=2, j=4)

            def col(v, c0, wd=128):
                return ("c", v[:, :, c0:c0 + wd], wd)

            def swqc(j):
                return ("q", swq[:, :, :, j, :], 128)
            groups_ab = []
            for hp in range(4):
                groups_ab.append([col(abv, hp * 128), col(abv, 512 + hp * 128), col(abv, 1024 + hp * 128)])
            groups_ab.append([swqc(0), swqc(1), swqc(2)])
            groups_ab.append([swqc(3), col(abv, 2048), col(abv, 2176)])
            groups_fx = []
            for hp in range(8):
                groups_fx.append([col(fxv, hp * 128), col(fxv, 1024 + hp * 128), col(fxv, 2048 + hp * 128)])
            groups_fx.append([col(fxv, 3072, 16)])
            for mi, groups in enumerate((groups_ab, groups_fx)):
                for gi, grp in enumerate(groups):
                    def loads(Fs, grp=grp):
                        fo = Fs[:, 0:3072].rearrange("p (kc c) -> p kc c", kc=8)
                        r = []
                        c0 = 0
                        for (kind, a, wd) in grp:
                            if kind == "c":
                                r.append((fo[:, :, c0:c0 + wd], a))
                            else:
                                for g_ in range(2):
                                    r.append((fo[:, :, c0 + g_ * 64:c0 + g_ * 64 + 64], a[:, :, g_, :]))
                            c0 += wd
                        return r
                    def copies(Fs, Bs, grp=grp):
                        return [(Bs[:, 0:3072], Fs[:, 0:3072])]
                    def stores(Bs, mi=mi, gi=gi, grp=grp):
                        return [(swm_d[mi][gi], Bs[:, 0:3072])]
                    conv(loads, copies, stores)
            for mi, wd_ in enumerate((about_d, fxout_d)):
                def loads(Fs, mi=mi, wd_=wd_):
                    fo = Fs[:, 0:8192].rearrange("p (o c) -> p o c", o=8)
                    if mi == 1:
                        return [(fo, wd_[0].rearrange("(o p) c -> p o c", p=128))]
                    r = [(fo[:, 0:4, :], wd_[0][0:512, :].rearrange("(o p) c -> p o c", p=128))]
                    for j in range(4):
                        r.append((fo[0:64, 4 + j, :], wd_[0][512 + 64 * j:512 + 64 * j + 64, :]))
                        r.append((fo[64:128, 4 + j, :], wd_[0][768 + 64 * j:768 + 64 * j + 64, :]))
                    return r
                def copies(Fs, Bs):
                    fi = Fs[:, 0:8192].rearrange("p (o d m) -> p o d m", o=8, d=8)
                    bo = Bs[:, 0:8192].rearrange("p (d o m) -> p d o m", d=8, o=8)
                    return [(bo[:, d, :, :], fi[:, :, d, :]) for d in range(8)]
                def stores(Bs, mi=mi):
                    bo = Bs[:, 0:8192].rearrange("p (d o m) -> p d o m", d=8, o=8)
                    return [(swp_d[mi].rearrange("d p o m -> p d o m"), bo)]
                conv(loads, copies, stores)

        def barrier():
            if ctx.dry:
                return
            evs = [(E.k, E.cnt) for E in (PE, ACT, DVE, POOL) if E.cnt > 0]
            evs += [(k, v) for k, v in enumerate(ctx.semval) if v > 0]
            for E in (PE, ACT, DVE, POOL, SP):
                for ev in evs:
                    if ev[0] != E.k:
                        E.wait(ev)

        def tiles_of(bl, c, tq):
            return bl[c][tq]

        def load_x(s):
            for tb in range(int(os.environ.get('KDBG_NTB', '16'))):
                tq = tb // 4
                si = tb % 3
                st = TFf[:, si * 1024:(si + 1) * 1024]
                stb = [B_tf[2 * si], B_tf[2 * si + 1]]
                dma(SP, k_xin[si], st, x_d[s, tb * 128:(tb + 1) * 128, :], W=stb)
                for half in range(0 if os.environ.get('KDBG_DMAONLY') else 2):
                    b = PS.get()
                    op(PE, [lambda c=c, b=b: mm(psum[b][:, (c % 4) * 128:(c % 4 + 1) * 128],
                                                lhsT=st[:, c * 128:(c + 1) * 128], rhs=identf[:], start=True, stop=True)
                            for c in range(half * 4, half * 4 + 4)], R=stb + [B_const], W=[B_ps[b]])
                    c0 = half * 4
                    op(ACT, [lambda c=c, b=b: act(out=XRES[:, c, tb * 128:(tb + 1) * 128], in_=psum[b][:, (c % 4) * 128:(c % 4 + 1) * 128], func=AF.Copy)
                             for c in range(c0, c0 + 4)],
                       R=[B_ps[b]], W=[B_xres[c][tq] for c in range(c0, c0 + 4)])
                    if not os.environ.get('KDBG_NODVE'):
                        op(DVE, [lambda c=c, b=b: nc.vector.tensor_copy(out=XB[:, c, tb * 128:(tb + 1) * 128], in_=psum[b][:, (c % 4) * 128:(c % 4 + 1) * 128])
                                 for c in range(c0, c0 + 4)],
                           R=[B_ps[b]], W=[B_xb[c][tq] for c in range(c0, c0 + 4)])

        def store_out(s):
            for tb in range(16):
                tq = tb // 4
                si = tb % 3
                st = TFf[:, si * 1024:(si + 1) * 1024]
                stb = [B_tf[2 * si], B_tf[2 * si + 1]]
                for half in range(2):
                    b = PS.get()
                    c0 = half * 4
                    op(PE, [lambda c=c, b=b: mm(psum[b][:, (c % 4) * 128:(c % 4 + 1) * 128],
                                                lhsT=XRES[:, c, tb * 128:(tb + 1) * 128], rhs=identf[:], start=True, stop=True)
                            for c in range(c0, c0 + 4)],
                       R=[B_xres[c][tq] for c in range(c0, c0 + 4)] + [B_const], W=[B_ps[b]])
                    if half == 0:
                        op(ACT, lambda b=b: act(out=st[:, 0:512], in_=psum[b][:], func=AF.Copy), R=[B_ps[b]], W=[stb[0]])
                    else:
                        op(DVE, lambda b=b: nc.vector.tensor_copy(out=st[:, 512:1024], in_=psum[b][:]), R=[B_ps[b]], W=[stb[1]])
                dma(SP, k_xout[si], out_d[s, tb * 128:(tb + 1) * 128, :], st, R=stb)

        def layer_norm(tq, li):
            ts = slice(tq * TT, (tq + 1) * TT)
            bM = PS.get(hold=True)
            bQ = PS.get(hold=True)
            for c in range(NCH):
                t1 = TBR.get()
                t2 = TBR.get()
                op(ACT, lambda c=c, t1=t1: act(out=TB[:, t1, :], in_=XRES[:, c, ts], func=AF.Copy),
                   R=[B_xres[c][tq]], W=[B_tb[t1]])
                op(ACT, lambda c=c, t2=t2: act(out=TB[:, t2, :], in_=XRES[:, c, ts], func=AF.Square),
                   R=[B_xres[c][tq]], W=[B_tb[t2]])
                op(PE, lambda c=c, t1=t1: mm(psum[bM][:], lhsT=ONESDIV, rhs=TB[:, t1, :], start=(c == 0), stop=(c == NCH - 1)),
                   R=[B_tb[t1], B_const], W=[B_ps[bM]])
                op(PE, lambda c=c, t2=t2: mm(psum[bQ][:], lhsT=ONESDIV, rhs=TB[:, t2, :], start=(c == 0), stop=(c == NCH - 1)),
                   R=[B_tb[t2], B_const], W=[B_ps[bQ]])
            f_msq = TFR.get(hold=True)
            op(ACT, lambda: act(out=TF[:, f_msq, :], in_=psum[bM][:], func=AF.Square), R=[B_ps[bM]], W=[B_tf[f_msq]])
            f_var = TFR.get(hold=True)
            op(DVE, lambda: nc.vector.tensor_tensor(out=TF[:, f_var, :], in0=psum[bQ][:], in1=TF[:, f_msq, :], op=ALU.subtract),
               R=[B_ps[bQ], B_tf[f_msq]], W=[B_tf[f_var]])
            op(ACT, lambda: act(out=TF[:, f_msq, :], in_=TF[:, f_var, :], func=AF.Sqrt, bias=cf[:, 1:2]),
               R=[B_tf[f_var], B_const], W=[B_tf[f_msq]])
            op(DVE, lambda: nc.vector.reciprocal(out=TF[:, f_var, :], in_=TF[:, f_msq, :]), R=[B_tf[f_msq]], W=[B_tf[f_var]])
            op(DVE, lambda: nc.vector.scalar_tensor_tensor(out=TF[:, f_msq, :], in0=psum[bM][:], scalar=-1.0, in1=TF[:, f_var, :],
                                                           op0=ALU.mult, op1=ALU.mult),
               R=[B_ps[bM], B_tf[f_var]], W=[B_tf[f_msq]])
            PS.release(bM)
            PS.release(bQ)
            for c in range(NCH):
                a = TFR.get()
                op(DVE, lambda c=c, a=a: nc.vector.tensor_tensor(out=TF[:, a, :], in0=XRES[:, c, ts], in1=TF[:, f_var, :], op=ALU.mult),
                   R=[B_xres[c][tq], B_tf[f_var]], W=[B_tf[a]])
                bb = TFR.get()
                op(DVE, lambda a=a, bb=bb: nc.vector.tensor_tensor(out=TF[:, bb, :], in0=TF[:, a, :], in1=TF[:, f_msq, :], op=ALU.add),
                   R=[B_tf[a], B_tf[f_msq]], W=[B_tf[bb]])
                gcol = lnp[:, li * 8 + c:li * 8 + c + 1]
                bcol = lnp[:, 48 + li * 8 + c:48 + li * 8 + c + 1]
                op(ACT, lambda c=c, bb=bb, gcol=gcol, bcol=bcol: act(out=XRES[:, c, ts], in_=TF[:, bb, :], func=AF.Identity,
                                                                      scale=gcol, bias=bcol),
                   R=[B_tf[bb], B_const], W=[B_xres[c][tq]])
                op(ACT, lambda c=c, bb=bb, gcol=gcol, bcol=bcol: act(out=XB[:, c, ts], in_=TF[:, bb, :], func=AF.Identity,
                                                                      scale=gcol, bias=bcol),
                   R=[B_tf[bb], B_const], W=[B_xb[c][tq]])
            TFR.release(f_msq)
            TFR.release(f_var)

        def ffn(l, w, tq, li):
            ts = slice(tq * TT, (tq + 1) * TT)
            xbR = [B_xb[c][tq] for c in range(NCH)]
            for j in range(NJ):
                if j % 2 == 0:
                    sl = ringA.get(lambda sl, u=j // 2: [(WA[:, sl, :], swi_d[l][w][u])])
                    wv = WA[:, sl, :].rearrange("p (kc g m) -> p kc g m", kc=8, g=2)
                jo = (j % 2) * 128
                bG = PS.get()
                bU = PS.get()
                op(PE, [lambda kc=kc, bG=bG, jo=jo, wv=wv: mm(psum[bG][:], lhsT=wv[:, kc, 0, jo:jo + 128], rhs=XB[:, kc, ts], start=(kc == 0), stop=(kc == 7))
                        for kc in range(8)], R=xbR + [B_wa[sl]], W=[B_ps[bG]])
                op(PE, [lambda kc=kc, bU=bU, jo=jo, wv=wv: mm(psum[bU][:], lhsT=wv[:, kc, 1, jo:jo + 128], rhs=XB[:, kc, ts], start=(kc == 0), stop=(kc == 7))
                        for kc in range(8)], R=xbR + [B_wa[sl]], W=[B_ps[bU]])
                a = TFR.get()
                op(ACT, lambda a=a, bG=bG: act(out=TF[:, a, :], in_=psum[bG][:], func=AF.Silu), R=[B_ps[bG]], W=[B_tf[a]])
                op(DVE, lambda a=a, bU=bU, j=j: nc.vector.scalar_tensor_tensor(out=GT[:, j, :], in0=TF[:, a, :], scalar=0.5, in1=psum[bU][:],
                                                               op0=ALU.mult, op1=ALU.mult),
                   R=[B_tf[a], B_ps[bU]], W=[B_biga[j]])
            for dc in range(NCH):
                sl0 = ringB.get(lambda sl, dc=dc: [(WB[:, sl, :, :], swo_d[l][w][dc][:, 0:11, :])])
                sl1 = ringB.get(lambda sl, dc=dc: [(WB[:, sl, :, :], swo_d[l][w][dc][:, 11:22, :])], keep=1)
                bY = PS.get()
                op(PE, [lambda fc=fc: mm(psum[bY][:], lhsT=WB[:, (sl0 if fc < 11 else sl1), fc % 11, :], rhs=GT[:, fc, :],
                                         start=(fc == 0), stop=(fc == NJ - 1))
                        for fc in range(NJ)], R=[B_biga[j] for j in range(NJ)] + [B_wb[sl0], B_wb[sl1]], W=[B_ps[bY]])
                op(DVE, lambda: nc.vector.scalar_tensor_tensor(out=XRES[:, dc, ts], in0=XRES[:, dc, ts], scalar=ALPHA, in1=psum[bY][:],
                                                               op0=ALU.mult, op1=ALU.add),
                   R=[B_ps[bY], B_xres[dc][tq]], W=[B_xres[dc][tq]])
            layer_norm(tq, li)

        def out_proj(mi, tq, li):
            ts = slice(tq * TT, (tq + 1) * TT)
            for dc in range(NCH):
                sl = ringB.get(lambda sl, dc=dc: [(WB[:, sl, 0:8, :], swp_d[mi][dc])])
                bY = PS.get()
                op(PE, [lambda oc=oc: mm(psum[bY][:], lhsT=WB[:, sl, oc, :], rhs=OT[:, oc, ts], start=(oc == 0), stop=(oc == 7))
                        for oc in range(8)], R=[B_biga[4 * oc + tq] for oc in range(8)] + [B_wb[sl]], W=[B_ps[bY]])
                op(DVE, lambda: nc.vector.scalar_tensor_tensor(out=XRES[:, dc, ts], in0=XRES[:, dc, ts], scalar=ALPHA, in1=psum[bY][:],
                                                               op0=ALU.mult, op1=ALU.add),
                   R=[B_ps[bY], B_xres[dc][tq]], W=[B_xres[dc][tq]])
            layer_norm(tq, li)

        def proj_fm(wsl, col0, dst_c, tq, scale, dst_view=None, wbuf=None):
            ts = slice(tq * TT, (tq + 1) * TT)
            wv = WA[:, wsl, 0:3072].rearrange("p (kc c) -> p kc c", kc=8)
            b = PS.get()
            op(PE, [lambda kc=kc: mm(psum[b][:], lhsT=wv[:, kc, col0:col0 + 128], rhs=XB[:, kc, ts], start=(kc == 0), stop=(kc == 7))
                    for kc in range(8)], R=[B_xb[c][tq] for c in range(NCH)] + [B_wa[wsl]], W=[B_ps[b]])
            return b

        def proj_v(wsl, col0, tq):
            wv = WA[:, wsl, 0:3072].rearrange("p (kc c) -> p kc c", kc=8)
            b = PS.get()
            fns = []
            for blk in range(4):
                tb = tq * 4 + blk
                for kc in range(8):
                    fns.append(lambda kc=kc, blk=blk, tb=tb: mm(psum[b][:, blk * 128:(blk + 1) * 128],
                                                                lhsT=XB[:, kc, tb * 128:(tb + 1) * 128],
                                                                rhs=wv[:, kc, col0:col0 + 128], start=(kc == 0), stop=(kc == 7)))
            op(PE, fns, R=[B_xb[c][tq] for c in range(NCH)] + [B_wa[wsl]], W=[B_ps[b]])
            vb = [B_va[tq * 4 + k] for k in range(4)]
            op(DVE, [lambda k=k: nc.vector.tensor_copy(out=VA[:, tq * 4 + k, 0, 0:64], in_=psum[b][:, k * 128:k * 128 + 64]) for k in range(4)],
               R=[B_ps[b]], W=vb)
            op(ACT, [lambda k=k: act(out=VA[:, tq * 4 + k, 1, 64:128], in_=psum[b][:, k * 128 + 64:k * 128 + 128], func=AF.Copy) for k in range(4)],
               R=[B_ps[b]], W=vb)

        def qk_proj_pair(wsl, qs, kcn):
            for tq in range(NT):
                ts = slice(tq * TT, (tq + 1) * TT)
                b = proj_fm(wsl, 0, qs, tq, 0.125)
                op(ACT, lambda b=b: act(out=QKB[:, qs, ts], in_=psum[b][:], func=AF.Copy, scale=0.125), R=[B_ps[b]], W=[B_qkb[qs][tq]])
                b2 = proj_fm(wsl, 128, kcn, tq, 1.0)
                op(DVE, lambda b2=b2: nc.vector.tensor_copy(out=QKB[:, kcn, ts], in_=psum[b2][:]), R=[B_ps[b2]], W=[B_qkb[kcn][tq]])
                proj_v(wsl, 256, tq)

        def sb_head(qs, kcn, hs, oc, tq):
            rows = slice(hs * 64, hs * 64 + 64)
            ts = slice(tq * TT, (tq + 1) * TT)
            nkb = 4 * tq + 4
            bO = PS.get(hold=True)
            bR = PS.get(hold=True)
            rs = None
            for idx, sbk in enumerate(range(nkb - 1, -1, -1)):
                r = sbk - 4 * tq
                bZ = PS.get()
                fns = [lambda: mm(psum[bZ][:], lhsT=QKB[rows, kcn, sbk * 128:(sbk + 1) * 128], rhs=QKB[rows, qs, ts],
                                  start=True, stop=(r < 0))]
                if r >= 0:
                    fns.append(lambda: mm(psum[bZ][:], lhsT=IDENTB, rhs=maskw[:, 384 - 128 * r:384 - 128 * r + 512],
                                          start=False, stop=True))
                op(PE, fns, R=[B_qkb[kcn][sbk // 4], B_qkb[qs][tq], B_const], W=[B_ps[bZ]])
                e = TFR.get()
                op(ACT, lambda: act(out=TF[:, e, :], in_=psum[bZ][:], func=AF.Exp), R=[B_ps[bZ]], W=[B_tf[e]])
                lb = TBR.get()
                op(ACT, lambda: act(out=TB[:, lb, :], in_=TF[:, e, :], func=AF.Ln, bias=cf[:, 0:1]), R=[B_tf[e], B_const], W=[B_tb[lb]])
                op(PE, lambda: mm(psum[bZ][:], lhsT=NEGTRI, rhs=TB[:, lb, :], start=False, stop=True, skip_group_check=True),
                   R=[B_tb[lb], B_const], W=[B_ps[bZ]])
                pt = TBR.get()
                if rs is None:
                    op(ACT, lambda: act(out=TB[:, pt, :], in_=psum[bZ][:], func=AF.Exp), R=[B_ps[bZ]], W=[B_tb[pt]])
                else:
                    a = TFR.get()
                    op(DVE, lambda: nc.vector.tensor_tensor(out=TF[:, a, :], in0=psum[bZ][:], in1=TF[:, rs, :], op=ALU.subtract),
                       R=[B_ps[bZ], B_tf[rs]], W=[B_tf[a]])
                    op(ACT, lambda: act(out=TB[:, pt, :], in_=TF[:, a, :], func=AF.Exp), R=[B_tf[a]], W=[B_tb[pt]])
                if sbk > 0:
                    op(PE, lambda: mm(psum[bR][:], lhsT=ONESB, rhs=TB[:, lb, :], start=(idx == 0), stop=True, skip_group_check=True),
                       R=[B_tb[lb], B_const], W=[B_ps[bR]])
                    if rs is not None:
                        TFR.release(rs)
                    rs = TFR.get(hold=True)
                    op(DVE, lambda rs=rs: nc.vector.tensor_copy(out=TF[:, rs, :], in_=psum[bR][:]), R=[B_ps[bR]], W=[B_tf[rs]])
                op(PE, lambda: mm(psum[bO][:], lhsT=VA[:, sbk, hs, :], rhs=TB[:, pt, :], start=(idx == 0), stop=(sbk == 0),
                                  skip_group_check=True),
                   R=[B_tb[pt], B_va[sbk]], W=[B_ps[bO]])
            if rs is not None:
                TFR.release(rs)
            op(DVE, lambda: nc.vector.tensor_copy(out=OT[rows, oc, ts], in_=psum[bO][rows, :]), R=[B_ps[bO]], W=[B_biga[4 * oc + tq]])
            PS.release(bO)
            PS.release(bR)

        def fox_head(qs, kcn, hs, h, oc, tq):
            rows = slice(hs * 64, hs * 64 + 64)
            drows = slice((1 - hs) * 64, (1 - hs) * 64 + 64)
            ts = slice(tq * TT, (tq + 1) * TT)
            nkb = 4 * tq + 4
            bO = PS.get(hold=True)
            for sbk in range(nkb):
                r = sbk - 4 * tq
                bZ = PS.get()
                fns = [lambda: mm(psum[bZ][:], lhsT=QKB[rows, kcn, sbk * 128:(sbk + 1) * 128], rhs=QKB[rows, qs, ts],
                                  start=True, stop=False),
                       lambda: mm(psum[bZ][:], lhsT=sel[0:80, h, :], rhs=CSf[0:80, ts], start=False, stop=(r < 0))]
                if r >= 0:
                    fns.append(lambda: mm(psum[bZ][:], lhsT=IDENTB, rhs=maskw[:, 385 - 128 * r:385 - 128 * r + 512],
                                          start=False, stop=True))
                op(PE, fns, R=[B_qkb[kcn][sbk // 4], B_qkb[qs][tq], B_const, B_cs[tq]], W=[B_ps[bZ]])
                pt = TBR.get()
                op(ACT, lambda: act(out=TB[:, pt, :], in_=psum[bZ][:], func=AF.Exp, bias=NEGC[:, sbk, h:h + 1]),
                   R=[B_ps[bZ], B_ls], W=[B_tb[pt]])
                op(PE, lambda: mm(psum[bO][:], lhsT=VA[:, sbk, hs, :], rhs=TB[:, pt, :], start=(sbk == 0), stop=(sbk == nkb - 1),
                                  skip_group_check=True),
                   R=[B_tb[pt], B_va[sbk]], W=[B_ps[bO]])
            rc = TFR.get()
            op(DVE, lambda: nc.vector.reciprocal(out=TF[drows, rc, :], in_=psum[bO][drows, :]), R=[B_ps[bO]], W=[B_tf[rc]])
            op(DVE, lambda: nc.vector.tensor_tensor(out=OT[rows, oc, ts], in0=psum[bO][rows, :], in1=TF[drows, rc, :], op=ALU.mult),
               R=[B_ps[bO], B_tf[rc]], W=[B_biga[4 * oc + tq]])
            PS.release(bO)

        def mixer_even(l, s):
            dma(SP, k_ls, LSf[:], swab_d, W=[B_ls] + B_cs)
            for hp in range(4):
                qs, kcn = (0, 1) if hp % 2 == 0 else (2, 3)
                wsl = ringA.get(lambda sl, hp=hp: [(WA[:, sl, 0:3072], swm_d[0][hp])])
                qk_proj_pair(wsl, qs, kcn)
                for tq in range(NT):
                    for hs in range(2):
                        sb_head(qs, kcn, hs, hp, tq)
            w1 = ringA.get(lambda sl: [(WA[:, sl, 0:3072], swm_d[0][4])])
            w2 = ringA.get(lambda sl: [(WA[:, sl, 0:3072], swm_d[0][5])], keep=1)
            KC = 0
            for tq in range(NT):
                ts = slice(tq * TT, (tq + 1) * TT)
                b2 = proj_fm(w2, 128, KC, tq, 1.0)
                op(DVE, lambda b2=b2: nc.vector.tensor_copy(out=QKB[:, KC, ts], in_=psum[b2][:]), R=[B_ps[b2]], W=[B_qkb[KC][tq]])
                proj_v(w2, 256, tq)
            for tq in range(NT):
                qc = 1 + (tq % 2)
                QT = QKB[:, qc, :].rearrange("p (j t) -> p j t", j=4)
                qbufs = [B_qkb[qc][j] for j in range(4)]
                for j in range(4):
                    b = proj_fm(w1 if j < 3 else w2, (j % 3) * 128 if j < 3 else 0, qc, tq, 0.125)
                    op(ACT, lambda b=b, j=j: act(out=QT[:, j, :], in_=psum[b][:], func=AF.Copy, scale=0.125), R=[B_ps[b]], W=[qbufs[j]])
                for qb in range(4 * tq, 4 * tq + 4):
                    ql = qb % 4
                    for g in range(2):
                        rows = slice(g * 64, g * 64 + 64)
                        drows = slice((1 - g) * 64, (1 - g) * 64 + 64)
                        bO = PS.get(hold=True)
                        kbs = [kb for kb in (qb - 1, qb) if kb >= 0]
                        for kb in kbs:
                            bZ = PS.get()
                            op(PE, lambda kb=kb, bZ=bZ: mm(psum[bZ][:].rearrange("p (j t) -> p j t", j=4),
                                                           lhsT=QKB[rows, KC, kb * 128:(kb + 1) * 128],
                                                           rhs=QT[rows, :, ql * 128:(ql + 1) * 128], start=True, stop=True),
                               R=[B_qkb[KC][kb // 4]] + qbufs, W=[B_ps[bZ]])
                            a = TFR.get()
                            ksel = 1 if kb == qb else 0
                            op(DVE, lambda a=a, bZ=bZ, ksel=ksel: nc.vector.tensor_tensor(
                                out=TF[:, a, :], in0=psum[bZ][:], in1=SWAB[:, g, ksel, :, :].rearrange("p j t -> p (j t)"), op=ALU.add),
                               R=[B_ps[bZ], B_ls], W=[B_tf[a]])
                            pt = TBR.get()
                            op(ACT, lambda a=a, pt=pt: act(out=TB[:, pt, :], in_=TF[:, a, :], func=AF.Exp), R=[B_tf[a]], W=[B_tb[pt]])
                            op(PE, lambda kb=kb, pt=pt: mm(psum[bO][:], lhsT=VA[:, kb, g, :], rhs=TB[:, pt, :], start=(kb == kbs[0]),
                                                           stop=(kb == kbs[-1]), skip_group_check=True),
                               R=[B_tb[pt], B_va[kb]], W=[B_ps[bO]])
                        d2 = TFR.get()
                        op(DVE, [lambda j=j: nc.vector.tensor_scalar(out=TF[drows, d2, j * 128:(j + 1) * 128], in0=psum[bO][drows, j * 128:(j + 1) * 128],
                                                                     scalar1=sinkexp[drows, 4 * g + j:4 * g + j + 1], scalar2=None, op0=ALU.add)
                                 for j in range(4)], R=[B_ps[bO], B_misc], W=[B_tf[d2]])
                        rc = TFR.get()
                        op(DVE, lambda: nc.vector.reciprocal(out=TF[drows, rc, :], in_=TF[drows, d2, :]), R=[B_tf[d2]], W=[B_tf[rc]])
                        op(DVE, [lambda j=j: nc.vector.tensor_tensor(out=OT[rows, 4 + j, qb * 128:(qb + 1) * 128],
                                                                     in0=psum[bO][rows, j * 128:(j + 1) * 128],
                                                                     in1=TF[drows, rc, j * 128:(j + 1) * 128], op=ALU.mult) for j in range(4)],
                           R=[B_ps[bO], B_tf[rc]], W=[B_biga[4 * oc + tq] for oc in range(4, 8)])
                        PS.release(bO)
            for tq in range(NT):
                out_proj(0, tq, l * 3 + 1)

        def mixer_odd(l, s):
            op(DVE, lambda: nc.vector.memset(CSf[:, :], 0.0), W=B_cs + [B_ls])
            dma(SP, k_ls, onesf, onesf_d, W=[B_ls])
            dma(SP, k_ls, trile, trile_d, W=[B_ls])
            wf = ringA.get(lambda sl: [(WA[:, sl, 0:3072], swm_d[1][8])])
            wv = WA[:, wf, 0:3072].rearrange("p (kc c) -> p kc c", kc=8)
            bF = PS.get(hold=True)
            fns = []
            for tb in range(16):
                for kc in range(8):
                    fns.append(lambda kc=kc, tb=tb: mm(psum[bF][:, tb * 16:(tb + 1) * 16], lhsT=XB[:, kc, tb * 128:(tb + 1) * 128],
                                                       rhs=wv[:, kc, 0:16], start=(kc == 0), stop=(kc == 7)))
            op(PE, fns, R=[B_xb[c][t] for c in range(NCH) for t in range(NT)] + [B_wa[wf]], W=[B_ps[bF]])
            a = TFR.get()
            op(DVE, [lambda tb=tb: nc.vector.tensor_tensor(out=TF[:, a, tb * 16:(tb + 1) * 16], in0=psum[bF][:, tb * 16:(tb + 1) * 16],
                                                           in1=bfb[:], op=ALU.add) for tb in range(16)],
               R=[B_ps[bF], B_const], W=[B_tf[a]])
            PS.release(bF)
            e = TFR.get()
            op(ACT, lambda: act(out=TF[:, e, 0:256], in_=TF[:, a, 0:256], func=AF.Exp, scale=-1.0), R=[B_tf[a]], W=[B_tf[e]])
            op(ACT, lambda: act(out=LSf[:, 512:768], in_=TF[:, e, 0:256], func=AF.Ln, bias=cf[:, 0:1]), R=[B_tf[e], B_const], W=[B_ls])
            bT = PS.get(hold=True)
            op(PE, lambda: mm(psum[bT][:, 0:256], lhsT=onesf, rhs=LSf[:, 512:768], start=True, stop=True), R=[B_ls, B_const], W=[B_ps[bT]])
            op(DVE, lambda: nc.vector.memset(LPRE[:, 0, :], 0.0), W=[B_ls])
            for b_ in range(1, 16):
                op(DVE, lambda b_=b_: nc.vector.tensor_tensor(out=LPRE[:, b_, :], in0=psum[bT][:, (b_ - 1) * 16:b_ * 16], in1=LPRE[:, b_ - 1, :],
                                                              op=ALU.add), R=[B_ps[bT], B_ls], W=[B_ls])
            PS.release(bT)
            bC = PS.get(hold=True)
            op(PE, [lambda b_=b_: mm(psum[bC][:, b_ * 16:(b_ + 1) * 16], lhsT=trile, rhs=LL[:, b_, :], start=True, stop=True)
                    for b_ in range(16)], R=[B_ls, B_const], W=[B_ps[bC]])
            op(DVE, lambda: nc.vector.tensor_tensor(out=LSf[:, 0:256], in0=psum[bC][:, 0:256], in1=LSf[:, 256:512], op=ALU.add),
               R=[B_ps[bC], B_ls], W=[B_ls])
            PS.release(bC)
            for tq in range(NT):
                ts = slice(tq * TT, (tq + 1) * TT)
                b = PS.get()
                op(PE, [lambda k=k: mm(psum[b][0:16, k * 128:(k + 1) * 128], lhsT=NEGC[:, tq * 4 + k, :], rhs=identf[:], start=True, stop=True)
                        for k in range(4)], R=[B_ls, B_const], W=[B_ps[b]])
                op(DVE, lambda: nc.vector.tensor_scalar(out=CSf[0:16, ts], in0=psum[b][0:16, :], scalar1=-1.0, scalar2=None, op0=ALU.mult),
                   R=[B_ps[b]], W=[B_cs[tq]])
                r1 = TFR.get()
                op(DVE, lambda: nc.vector.scalar_tensor_tensor(out=TF[0:16, r1, :], in0=psum[b][0:16, :], scalar=-1.0, in1=CSf[0:16, ts],
                                                               op0=ALU.mult, op1=ALU.subtract), R=[B_ps[b], B_cs[tq]], W=[B_tf[r1]])
                lt = TBR.get()
                op(DVE, lambda: nc.vector.tensor_copy(out=TB[0:16, lt, :], in_=TF[0:16, r1, :]), R=[B_tf[r1]], W=[B_tb[lt]])
                op(DVE, lambda: nc.vector.tensor_copy(out=CSf[32:48, ts], in_=TB[0:16, lt, :]), R=[B_tb[lt]], W=[B_cs[tq]])
                r2 = TFR.get()
                op(DVE, lambda: nc.vector.tensor_tensor(out=TF[0:16, r2, :], in0=TF[0:16, r1, :], in1=TB[0:16, lt, :], op=ALU.subtract),
                   R=[B_tf[r1], B_tb[lt]], W=[B_tf[r2]])
                op(DVE, lambda: nc.vector.tensor_copy(out=CSf[64:80, ts], in_=TF[0:16, r2, :]), R=[B_tf[r2]], W=[B_cs[tq]])
            for hp in range(8):
                qs, kcn = (0, 1) if hp % 2 == 0 else (2, 3)
                wsl = ringA.get(lambda sl, hp=hp: [(WA[:, sl, 0:3072], swm_d[1][hp])])
                qk_proj_pair(wsl, qs, kcn)
                for tq in range(NT):
                    for hs in range(2):
                        fox_head(qs, kcn, hs, 2 * hp + hs, hp, tq)
            for tq in range(NT):
                out_proj(1, tq, l * 3 + 1)

        def program():
            PS.i = 0
            load_consts()
            if not os.environ.get('KDBG_NOPRO'):
                prologue()
            barrier()
            stage = 0
            dbg = os.environ.get('KDBG_STAGE', '')
            for s in range(nseq):
                if dbg != 'c':
                    load_x(s)
                for l in range(2):
                    if upto >= 3 * l + 1:
                        for tq in range(NT):
                            ffn(l, 0, tq, l * 3 + 0)
                    if upto >= 3 * l + 2:
                        if l == 0:
                            mixer_even(l, s)
                        else:
                            mixer_odd(l, s)
                    if upto >= 3 * l + 3:
                        for tq in range(NT):
                            ffn(l, 1, tq, l * 3 + 2)
                if dbg not in ('c', 'x'):
                    store_out(s)
            if not ctx.dry:
                for k in k_xout + k_xin:
                    if ctx.semval[k] > 0:
                        SP.wait((k, ctx.semval[k]))

        ctx.dry = True
        program()
        ctx.dry = False
        for R_ in (PS, TFR, TBR):
            R_.i = 0
            R_.held = set()
        conv_state["i"] = 0
        conv_state["e"] = 0
        ringA.reset()
        ringB.reset()
        program()
        print("instructions:", ctx.n_ins, "sems:", len(ctx.sems),
              "counts:", {E.name: E.cnt for E in (PE, ACT, DVE, POOL)})
    return nc


def _t5_bucket(rel):
    n = np.maximum(rel, 0)
    max_exact = 16
    nf = np.maximum(n, 1).astype(np.float32)
    large = max_exact + (np.log(nf / max_exact) / np.log(128 / max_exact) * (32 - max_exact)).astype(np.int32)
    large = np.minimum(large, 31)
    return np.where(n < max_exact, n, large)


def _host_tables(ln_g, ln_b, ab_sinks, fox_b_f, rel_bias):
    bf = ml_dtypes.bfloat16
    t = {}
    lnp = np.zeros((128, 96), np.float32)
    lnp[:, 0:48] = ln_g.reshape(6, 8, 128).transpose(2, 0, 1).reshape(128, 48)
    lnp[:, 48:96] = ln_b.reshape(6, 8, 128).transpose(2, 0, 1).reshape(128, 48)
    t["lnp"] = lnp
    cfv = np.zeros((128, 8), np.float32)
    cfv[:, 0] = 1.0
    cfv[:, 1] = EPS
    t["cf"] = cfv
    t["identf"] = np.eye(128, dtype=np.float32)
    ii = np.arange(128)
    t["trile"] = (ii[:, None] <= ii[None, :]).astype(np.float32)
    t["onesf"] = np.ones((128, 128), np.float32)
    cb = np.zeros((128, 4, 128), np.float32)
    cb[:, 0, :] = 1.0 / 1024.0
    cb[:, 1, :] = np.eye(128)
    cb[:, 2, :] = -(ii[:, None] >= ii[None, :]).astype(np.float32)
    cb[:, 3, :] = 1.0
    t["cbf"] = cb.astype(bf)
    v = np.arange(897)
    t["maskw"] = np.where(ii[:, None] <= v[None, :] - 385, 0.0, NEG).astype(np.float32).astype(bf)
    sel = np.zeros((128, 16, 128), np.float32)
    for h in range(16):
        for base in (0, 32, 64):
            sel[base + h, h, :] = 1.0
    t["sel"] = sel.astype(bf)
    s_ = np.arange(128)[:, None]
    t_ = np.arange(128)[None, :]
    swab = np.zeros((128, 2, 2, 4, 128), np.float32)
    for ksel in range(2):
        rel = t_ - s_ + (128 if ksel == 0 else 0)
        valid = (rel >= 0) & (rel < 128)
        bk = _t5_bucket(rel)
        for g in range(2):
            for j in range(4):
                hq = 4 * g + j
                swab[:, g, ksel, j, :] = np.where(valid, rel_bias[bk, hq], NEG)
    t["swab"] = swab.reshape(128, 2048)
    t["sinkb"] = np.broadcast_to(ab_sinks.reshape(1, 8), (128, 8)).astype(np.float32).copy()
    t["bfb"] = np.broadcast_to(fox_b_f.reshape(1, 16), (128, 16)).astype(np.float32).copy()
    return t


_NC_CACHE = {}


def kernel(x, ln_g, ln_b, ffn1_in, ffn1_out, ffn2_in, ffn2_out, ab_w_in, ab_w_out,
           ab_sinks, fox_w_in, fox_b_f, fox_w_out, rel_bias, _nseq=2, _upto=99, _ncores=N_CORES):
    f = lambda a: np.ascontiguousarray(np.asarray(a, dtype=np.float32))
    x = f(x)
    tabs = _host_tables(f(ln_g), f(ln_b), f(ab_sinks), f(fox_b_f), f(rel_bias))
    shared = {"ffn1_in": f(ffn1_in), "ffn1_out": f(ffn1_out), "ffn2_in": f(ffn2_in), "ffn2_out": f(ffn2_out),
              "ab_w_in": f(ab_w_in), "ab_w_out": f(ab_w_out), "fox_w_in": f(fox_w_in), "fox_w_out": f(fox_w_out)}
    shared.update(tabs)
    key = (_nseq, _upto)
    if key not in _NC_CACHE:
        _NC_CACHE[key] = build(_nseq, _upto)
    nc = _NC_CACHE[key]
    in_maps = []
    for c in range(_ncores):
        m = dict(shared)
        m["x"] = np.ascontiguousarray(x[c * _nseq:(c + 1) * _nseq])
        in_maps.append(m)
    res = run_bass_kernel_spmd(nc, in_maps, core_ids=list(range(_ncores)))
    outs = [np.asarray(r["out"], dtype=np.float32) for r in res.results]
    return np.concatenate(outs, axis=0)
```
